# Optimizing a Trainium2 kernel written in Bass

```python
import math
import jax, jax.numpy as jnp
from jax import lax
import numpy as np

D_MODEL = 2048
BATCH = 8
SEQ = 4096
DEPTH = 4

GRID_W = 64
CTX_LEN = 256
EPS = 1e-6
ROPE_BASE = 10000.0
CHUNK = 64
Q_BLOCK = 128

RET_HEADS = 4
RET_HD = 128
D_RET = RET_HEADS * RET_HD
GDN_HEADS = 4
GDN_HD = 128
D_GDN = GDN_HEADS * GDN_HD
GDN_CONV = 5
MLA_HEADS = 8
MLA_NOPE = 128
MLA_ROPE = 64
MLA_VD = 128
MLA_Q_RANK = 768
MLA_KV_RANK = 512
D_MLA = MLA_HEADS * MLA_VD
MLA_SCALE = (MLA_NOPE + MLA_ROPE) ** -0.5

D_MIX = D_RET + D_GDN + D_MLA

IN_SPLITS = (D_RET, D_RET, D_RET, D_RET,
             D_GDN, D_GDN, D_GDN, D_GDN, 2 * GDN_HEADS, 2 * GDN_HEADS,
             MLA_Q_RANK, MLA_KV_RANK, MLA_ROPE, D_MLA)
D_IN = sum(IN_SPLITS)

kernel_name = 'hybrid_ret_gdn_mla_prefix_dit'


def rmsnorm(x, w):
    xf = x.astype(jnp.float32)
    y = xf * lax.rsqrt(jnp.mean(xf * xf, axis=-1, keepdims=True) + EPS)
    return (y * w.astype(jnp.float32)).astype(x.dtype)


def l2norm(x):
    xf = x.astype(jnp.float32)
    return (xf * lax.rsqrt(jnp.sum(xf * xf, axis=-1, keepdims=True) + EPS)).astype(x.dtype)


def split_cols(z, sizes):
    return jnp.split(z, np.cumsum(sizes)[:-1].tolist(), axis=-1)


def heads(t, n_heads):
    return t.reshape(t.shape[:-1] + (n_heads, -1))


def to_bhld(t):
    return jnp.transpose(t, (0, 2, 1, 3))


def merge_heads(o):
    B, H, L, d = o.shape
    return jnp.transpose(o, (0, 2, 1, 3)).reshape(B, L, H * d)


def flip_seq(t):
    return None if t is None else jnp.flip(t, axis=2)


def axial_rope_tables(L, dim):
    rows = L // GRID_W
    row = jnp.repeat(jnp.arange(rows), GRID_W).astype(jnp.float32)
    col = jnp.tile(jnp.arange(GRID_W), rows).astype(jnp.float32)
    m = dim // 2
    inv = ROPE_BASE ** (-jnp.arange(m // 2, dtype=jnp.float32) / (m // 2))
    ar = row[:, None] * inv
    ac = col[:, None] * inv
    return (jnp.cos(ar), jnp.sin(ar), jnp.cos(ac), jnp.sin(ac))


def rope_half(x, cos, sin):
    x1, x2 = jnp.split(x, 2, axis=-1)
    c = cos[:, None, :].astype(x.dtype)
    s = sin[:, None, :].astype(x.dtype)
    return jnp.concatenate([x1 * c - x2 * s, x1 * s + x2 * c], axis=-1)


def axial_rope(x, rope):
    cos_r, sin_r, cos_c, sin_c = rope
    xr, xc = jnp.split(x, 2, axis=-1)
    return jnp.concatenate([rope_half(xr, cos_r, sin_r), rope_half(xc, cos_c, sin_c)], axis=-1)


def centred_dwconv(x, w):
    K = w.shape[0]
    return lax.conv_general_dilated(
        x, w[:, None, :].astype(x.dtype), window_strides=(1,), padding=[(K // 2, K // 2)],
        dimension_numbers=('NWC', 'WIO', 'NWC'), feature_group_count=x.shape[-1])


def chunk_scan(q, k, v, g, beta, s0):
    out_dtype = v.dtype
    f32 = jnp.float32
    q, k, v, g = (t.astype(f32) for t in (q, k, v, g))
    B, H, L, dk = q.shape
    dv = v.shape[-1]
    n = L // CHUNK
    rs = lambda t: t.reshape((B, H, n, CHUNK) + t.shape[3:])
    q, k, v, g = rs(q), rs(k), rs(v), rs(g)
    gc = jnp.cumsum(g, axis=-1)
    incl = jnp.tril(jnp.ones((CHUNK, CHUNK), bool))
    decay = jnp.exp(jnp.where(incl, gc[..., :, None] - gc[..., None, :], -jnp.inf))
    if beta is None:
        u, w = v, None
    else:
        beta = rs(beta.astype(f32))
        kb = k * beta[..., None]
        strict = jnp.tril(jnp.ones((CHUNK, CHUNK), bool), -1)
        a = jnp.where(strict, jnp.einsum('bhncd,bhnsd->bhncs', kb, k) * decay, 0.0)
        rhs = jnp.concatenate([v * beta[..., None], kb * jnp.exp(gc)[..., None]], axis=-1)
        sol = lax.linalg.triangular_solve(a + jnp.eye(CHUNK, dtype=f32), rhs, left_side=True,
                                          lower=True, unit_diagonal=True)
        u, w = sol[..., :dv], sol[..., dv:]
    att = jnp.einsum('bhncd,bhnsd->bhncs', q, k) * decay
    mv = lambda t: None if t is None else jnp.moveaxis(t, 2, 0)

    def step(s, xs):
        q_i, k_i, u_i, w_i, gc_i, att_i = xs
        v_new = u_i if w_i is None else u_i - jnp.einsum('bhcd,bhde->bhce', w_i, s)
        o_i = (jnp.einsum('bhcd,bhde->bhce', q_i * jnp.exp(gc_i)[..., None], s)
               + jnp.einsum('bhcs,bhse->bhce', att_i, v_new))
        g_last = gc_i[..., -1]
        k_dec = k_i * jnp.exp(g_last[..., None] - gc_i)[..., None]
        s = s * jnp.exp(g_last)[..., None, None] + jnp.einsum('bhcd,bhce->bhde', k_dec, v_new)
        return s, o_i

    s, o = lax.scan(step, s0.astype(f32), (mv(q), mv(k), mv(u), mv(w), mv(gc), mv(att)))
    o = jnp.moveaxis(o, 0, 2).reshape(B, H, L, dv)
    return o.astype(out_dtype), s


def bidir_scan(ctx_qkv, lat_qkv, g_ctx, g_lat, beta_ctx, beta_lat):
    qc, kc, vc = ctx_qkv
    B, H, _, dk = qc.shape
    s0 = jnp.zeros((B, H, dk, vc.shape[-1]), jnp.float32)
    outs_c, outs_l = [], []
    for d in range(2):
        fl = (lambda t: t) if d == 0 else flip_seq
        oc, s_ctx = chunk_scan(*(fl(t) for t in ctx_qkv), fl(g_ctx[d]), fl(beta_ctx[d]), s0)
        ol, _ = chunk_scan(*(fl(t) for t in lat_qkv), fl(g_lat[d]), fl(beta_lat[d]), s_ctx)
        outs_c.append(fl(oc))
        outs_l.append(fl(ol))
    return outs_c[0] + outs_c[1], outs_l[0] + outs_l[1]


def retention_mixer(zc, zl, log_decay, norm_w, rope, need_ctx):
    def prep(q, k, v, rot):
        q, k, v = (heads(t, RET_HEADS) for t in (q, k, v))
        if rot is not None:
            q, k = axial_rope(q, rot), axial_rope(k, rot)
        return to_bhld(q), to_bhld(k * RET_HD ** -0.5), to_bhld(v)

    ctx_qkv = prep(zc[0], zc[1], zc[2], None)
    lat_qkv = prep(zl[0], zl[1], zl[2], rope)
    B = lat_qkv[0].shape[0]
    decays = lambda n: tuple(jnp.broadcast_to(log_decay[d][None, :, None], (B, RET_HEADS, n))
                             for d in range(2))
    o_c, o_l = bidir_scan(ctx_qkv, lat_qkv, decays(ctx_qkv[0].shape[2]), decays(lat_qkv[0].shape[2]),
                          (None, None), (None, None))
    out = lambda o, gate: jax.nn.silu(gate) * merge_heads(rmsnorm(o, norm_w))
    return (out(o_c, zc[3]) if need_ctx else None), out(o_l, zl[3])


def gated_deltanet_mixer(zc, zl, conv_w, a_log, dt_bias, norm_w, need_ctx):
    def prep(q, k, v, a, b):
        qkv = jax.nn.silu(centred_dwconv(jnp.concatenate([q, k, v], axis=-1), conv_w))
        q, k, v = (heads(t, GDN_HEADS) for t in jnp.split(qkv, 3, axis=-1))
        q = l2norm(q) * GDN_HD ** -0.5
        k = l2norm(k)
        Bn, n = a.shape[:2]
        a = a.astype(jnp.float32).reshape(Bn, n, 2, GDN_HEADS)
        b = b.astype(jnp.float32).reshape(Bn, n, 2, GDN_HEADS)
        g = -jnp.exp(a_log.astype(jnp.float32)) * jax.nn.softplus(a + dt_bias.astype(jnp.float32))
        beta = jax.nn.sigmoid(b)
        per_dir = lambda t: tuple(jnp.transpose(t[:, :, d], (0, 2, 1)) for d in range(2))
        return (to_bhld(q), to_bhld(k), to_bhld(v)), per_dir(g), per_dir(beta)

    ctx_qkv, g_c, b_c = prep(zc[0], zc[1], zc[2], zc[4], zc[5])
    lat_qkv, g_l, b_l = prep(zl[0], zl[1], zl[2], zl[4], zl[5])
    o_c, o_l = bidir_scan(ctx_qkv, lat_qkv, g_c, g_l, b_c, b_l)
    out = lambda o, gate: jax.nn.silu(gate) * merge_heads(rmsnorm(o, norm_w))
    return (out(o_c, zc[3]) if need_ctx else None), out(o_l, zl[3])


def sdpa(q, k, v):
    s = jnp.einsum('bhqd,bhkd->bhqk', q, k).astype(jnp.float32) * MLA_SCALE
    p = jax.nn.softmax(s, axis=-1).astype(v.dtype)
    return jnp.einsum('bhqk,bhkd->bhqd', p, v)


def blocked_attention(q, k, v):
    B, H, L, d = q.shape
    nb = L // Q_BLOCK
    qb = jnp.moveaxis(q.reshape(B, H, nb, Q_BLOCK, d), 2, 0)
    o = lax.map(lambda qi: sdpa(qi, k, v), qb)
    return jnp.moveaxis(o, 0, 2).reshape(B, H, L, v.shape[-1])


def mla_mixer(zc, zl, q_norm_w, w_qb, kv_norm_w, w_kvb, rope, need_ctx):
    def prep(cq, ckv, kr, rot):
        q = heads(rmsnorm(cq, q_norm_w) @ w_qb, MLA_HEADS)
        kv = heads(rmsnorm(ckv, kv_norm_w) @ w_kvb, MLA_HEADS)
        q_nope, q_rope = q[..., :MLA_NOPE], q[..., MLA_NOPE:]
        k_nope, v = kv[..., :MLA_NOPE], kv[..., MLA_NOPE:]
        kr = kr[:, :, None, :]
        if rot is not None:
            q_rope, kr = axial_rope(q_rope, rot), axial_rope(kr, rot)
        k = jnp.concatenate([k_nope, jnp.broadcast_to(kr, k_nope.shape[:-1] + (MLA_ROPE,))], axis=-1)
        q = jnp.concatenate([q_nope, q_rope], axis=-1)
        return to_bhld(q), to_bhld(k), to_bhld(v)

    qc, kc, vc = prep(zc[0], zc[1], zc[2], None)
    ql, kl, vl = prep(zl[0], zl[1], zl[2], rope)
    k_all = jnp.concatenate([kc, kl], axis=2)
    v_all = jnp.concatenate([vc, vl], axis=2)
    o_l = jax.nn.silu(zl[3]) * merge_heads(blocked_attention(ql, k_all, v_all))
    o_c = jax.nn.silu(zc[3]) * merge_heads(sdpa(qc, kc, vc)) if need_ctx else None
    return o_c, o_l


def setup_inputs(seed: int = 0) -> dict:
    key = jax.random.key(seed)
    ks = jax.random.split(key, 20)
    f32 = jnp.float32

    def normal(k, shape, scale):
        return jax.random.normal(k, shape, f32) * scale

    def gain(k, shape):
        return 1.0 + 0.02 * jax.random.normal(k, shape, f32)

    base_decay = jnp.log(1.0 - 2.0 ** (-5.0 - jnp.arange(RET_HEADS, dtype=f32)))
    ret_log_decay = base_decay * jnp.exp(0.1 * jax.random.normal(ks[8], (DEPTH, 2, RET_HEADS), f32))
    dt = jnp.exp(jax.random.uniform(ks[10], (DEPTH, 2, GDN_HEADS), f32, math.log(1e-3), math.log(1e-1)))
    gdn_dt_bias = dt + jnp.log(-jnp.expm1(-dt))
    gdn_a_log = jnp.log(jax.random.uniform(ks[11], (DEPTH, 2, GDN_HEADS), f32, 1.0, 16.0))
    return {
        'x': normal(ks[0], (BATCH, SEQ, D_MODEL), 1.0),
        'c': normal(ks[1], (BATCH, D_MODEL), 1.0),
        'ctx': normal(ks[2], (BATCH, CTX_LEN, D_MODEL), 1.0),
        'c_ctx': normal(ks[3], (D_MODEL,), 1.0),
        'ada_w': normal(ks[4], (DEPTH, D_MODEL, 3 * D_MODEL), 0.5 * D_MODEL ** -0.5),
        'ada_b': normal(ks[5], (DEPTH, 3 * D_MODEL), 0.02),
        'norm_w': gain(ks[6], (DEPTH, D_MODEL)),
        'w_in': normal(ks[7], (DEPTH, D_MODEL, D_IN), D_MODEL ** -0.5),
        'ret_log_decay': ret_log_decay,
        'ret_norm_w': gain(ks[9], (DEPTH, RET_HD)),
        'gdn_conv_w': normal(ks[12], (DEPTH, GDN_CONV, 3 * D_GDN), GDN_CONV ** -0.5),
        'gdn_a_log': gdn_a_log,
        'gdn_dt_bias': gdn_dt_bias,
        'gdn_norm_w': gain(ks[13], (DEPTH, GDN_HD)),
        'mla_q_norm_w': gain(ks[14], (DEPTH, MLA_Q_RANK)),
        'mla_w_qb': normal(ks[15], (DEPTH, MLA_Q_RANK, MLA_HEADS * (MLA_NOPE + MLA_ROPE)), MLA_Q_RANK ** -0.5),
        'mla_kv_norm_w': gain(ks[16], (DEPTH, MLA_KV_RANK)),
        'mla_w_kvb': normal(ks[17], (DEPTH, MLA_KV_RANK, MLA_HEADS * (MLA_NOPE + MLA_VD)), MLA_KV_RANK ** -0.5),
        'w_out': normal(ks[18], (DEPTH, D_MIX, D_MODEL), D_MIX ** -0.5),
        'final_norm_w': gain(ks[19], (D_MODEL,)),
    }


def reference(x, c, ctx, c_ctx, ada_w, ada_b, norm_w, w_in, ret_log_decay, ret_norm_w,
              gdn_conv_w, gdn_a_log, gdn_dt_bias, gdn_norm_w, mla_q_norm_w, mla_w_qb,
              mla_kv_norm_w, mla_w_kvb, w_out, final_norm_w):
    L = x.shape[1]
    rope_ret = axial_rope_tables(L, RET_HD)
    rope_mla = axial_rope_tables(L, MLA_ROPE)
    xc = ctx
    for i in range(DEPTH):
        last = i == DEPTH - 1
        shift, scale, gate = jnp.split(jax.nn.silu(c) @ ada_w[i] + ada_b[i], 3, axis=-1)
        shift_c, scale_c, gate_c = jnp.split(jax.nn.silu(c_ctx) @ ada_w[i] + ada_b[i], 3, axis=-1)
        h = rmsnorm(x, norm_w[i]) * (1.0 + scale[:, None]) + shift[:, None]
        hc = rmsnorm(xc, norm_w[i]) * (1.0 + scale_c) + shift_c
        zl = split_cols(h @ w_in[i], IN_SPLITS)
        zc = split_cols(hc @ w_in[i], IN_SPLITS)
        ret_c, ret_l = retention_mixer(zc[0:4], zl[0:4], ret_log_decay[i], ret_norm_w[i], rope_ret, not last)
        gdn_c, gdn_l = gated_deltanet_mixer(zc[4:10], zl[4:10], gdn_conv_w[i], gdn_a_log[i],
                                            gdn_dt_bias[i], gdn_norm_w[i], not last)
        mla_c, mla_l = mla_mixer(zc[10:14], zl[10:14], mla_q_norm_w[i], mla_w_qb[i], mla_kv_norm_w[i],
                                 mla_w_kvb[i], rope_mla, not last)
        x = x + gate[:, None] * (jnp.concatenate([ret_l, gdn_l, mla_l], axis=-1) @ w_out[i])
        if not last:
            xc = xc + gate_c * (jnp.concatenate([ret_c, gdn_c, mla_c], axis=-1) @ w_out[i])
    return rmsnorm(x, final_norm_w)
```

```python
import contextlib
import math
import numpy as np
import concourse.bass as bass
import concourse.mybir as mybir
from concourse.bass_utils import run_bass_kernel_spmd

F32 = mybir.dt.float32
BF16 = mybir.dt.bfloat16
AF = mybir.ActivationFunctionType
ALU = mybir.AluOpType
EPS = 1e-6
STOP = {}


class Cfg:
    def __init__(self, D=2048, SEQ=4096, CTX=256, DEPTH=4, RH=4, GH=4, MH=8, QR=768, KVR=512):
        self.D, self.SEQ, self.CTX, self.DEPTH = D, SEQ, CTX, DEPTH
        self.RH, self.GH, self.MH, self.QR, self.KVR = RH, GH, MH, QR, KVR
        self.T = CTX + SEQ
        self.KD = D // 128
        self.DRET, self.DGDN, self.DMLA = RH * 128, GH * 128, MH * 128
        self.DMIX = self.DRET + self.DGDN + self.DMLA
        sp = [self.DRET] * 4 + [self.DGDN] * 4 + [2 * GH, 2 * GH, QR, KVR, 64, self.DMLA]
        self.off = np.concatenate([[0], np.cumsum(sp)]).tolist()
        self.DIN = self.off[-1]
        self.blocks = []
        t = 0
        while t < CTX:
            s = min(512, CTX - t); self.blocks.append((t, s, 1)); t += s
        while t < self.T:
            s = min(512, self.T - t); self.blocks.append((t, s, 0)); t += s
        self.NCH = self.T // 128
        self.NCC = CTX // 128


class TL:
    __slots__ = ("sem", "step", "n")

    def __init__(self, sem, step):
        self.sem, self.step, self.n = sem, step, 0


class Res:
    __slots__ = ("w", "r")

    def __init__(self):
        self.w, self.r = None, {}


class Tile:
    def __init__(self, t, res=None, psum=False):
        self.t, self.res, self.psum = t, res or Res(), psum

    def __getitem__(self, k):
        return self.t[k]


class Eng:
    def __init__(self, name, eng):
        self.name, self.eng, self.tl, self.waited = name, eng, None, {}


class Em:
    LIM = 30000

    def __init__(self, nc, es):
        self.nc, self.es, self.ns = nc, es, 0
        self.eng = {n: Eng(n, getattr(nc, n)) for n in ("tensor", "vector", "scalar", "gpsimd", "sync")}
        for e in self.eng.values():
            e.tl = self.newtl(1)
        self.slots = {q: [self.newtl(16) for _ in range(6)] for q in ("sync", "gpsimd")}
        self.si = {"sync": 0, "gpsimd": 0}
        self.ninst = 0

    def newtl(self, step):
        self.ns += 1
        return TL(self.es.enter_context(self.nc.semaphore(f"sem{self.ns}")), step)

    def _res(self, x):
        return x.res if isinstance(x, Tile) else x

    def _wait(self, E, deps):
        for tl, n in deps:
            if tl is E.tl and E.name == "tensor":
                continue
            if E.waited.get(tl, 0) < n:
                E.eng.wait_ge(tl.sem, n * tl.step)
                E.waited[tl] = n

    def _deps(self, reads, writes):
        d = {}
        for r in reads:
            r = self._res(r)
            if r.w is not None:
                d[r.w[0]] = max(d.get(r.w[0], 0), r.w[1])
        for w in writes:
            w = self._res(w)
            if w.w is not None:
                d[w.w[0]] = max(d.get(w.w[0], 0), w.w[1])
            for tl, n in w.r.items():
                d[tl] = max(d.get(tl, 0), n)
        return d.items()

    def _commit(self, tl, n, reads, writes):
        for r in reads:
            r = self._res(r)
            r.r[tl] = n
        for w in writes:
            w = self._res(w)
            w.w, w.r = (tl, n), {}

    def op(self, en, fn, reads=(), writes=()):
        E = self.eng[en]
        pr = [r for r in reads if isinstance(r, Tile) and r.psum]
        if pr:
            writes = list(writes) + pr
            reads = [r for r in reads if not (isinstance(r, Tile) and r.psum)]
        if E.tl.n >= self.LIM:
            E.tl = self.newtl(1)
        self._wait(E, self._deps(reads, writes))
        ins = fn(E.eng)
        E.tl.n += 1
        ins.then_inc(E.tl.sem, 1)
        self._commit(E.tl, E.tl.n, reads, writes)
        self.ninst += 1

    def dma(self, q, out, in_, reads=(), writes=(), **kw):
        E = self.eng[q]
        sl = self.slots[q]
        i = self.si[q] % len(sl)
        self.si[q] += 1
        if sl[i].n * 16 >= self.LIM:
            sl[i] = self.newtl(16)
        tl = sl[i]
        nd = lambda xs: [x for x in xs if not getattr(x, "dram", False)]
        deps = dict(self._deps(nd(reads), nd(writes)))
        if tl.n > 0:
            deps[tl] = max(deps.get(tl, 0), tl.n)
        self._wait(E, deps.items())
        ins = E.eng.dma_start(out=out, in_=in_, **kw)
        tl.n += 1
        ins.then_inc(tl.sem, 16)
        self._commit(tl, tl.n, reads, writes)
        self.ninst += 1

    def barrier(self):
        tls = [e.tl for e in self.eng.values()] + [t for s in self.slots.values() for t in s]
        for E in self.eng.values():
            self._wait(E, [(t, t.n) for t in tls if t.n > 0])

    def final_wait(self, res):
        res = self._res(res)
        self._wait(self.eng["sync"], [res.w])


_UID = [0]


def uid():
    _UID[0] += 1
    return _UID[0]


class Pool:
    def __init__(self, em, es, name, shape, dtype, bufs):
        u = uid()
        self.tiles = [Tile(es.enter_context(em.nc.sbuf_tensor(f"pl{u}_{name}_{i}", list(shape), dtype))) for i in range(bufs)]
        self.i = 0

    def get(self):
        t = self.tiles[self.i % len(self.tiles)]
        self.i += 1
        return t


def sb(em, es, name, shape, dtype):
    return Tile(es.enter_context(em.nc.sbuf_tensor(f"sb{uid()}_{name}", list(shape), dtype)))


class Prog:
    def __init__(self, cfg, debug=()):
        self.cfg = cfg
        self.debug = set(debug)
        self.nc = bass.Bass("TRN2", target_bir_lowering=False)
        self.ins = {}
        self.scr = {}

    def inp(self, name, shape, dtype=F32):
        self.ins[name] = self.nc.dram_tensor(name, list(shape), dtype, kind="ExternalInput").ap()
        return self.ins[name]

    def scratch(self, name, shape, dtype):
        kind = "ExternalOutput" if name in self.debug else "Internal"
        t = Tile(self.nc.dram_tensor(name, list(shape), dtype, kind=kind).ap())
        t.dram = True
        self.scr[name] = t
        return t

    def psum(self):
        t = self.ps[self.psi % len(self.ps)]
        self.psi += 1
        return t

    def build(self):
        cfg, nc = self.cfg, self.nc
        L, D, T, KD = cfg.DEPTH, cfg.D, cfg.T, cfg.KD
        I = self.inp
        I("xT", [D, T]); I("cc", [128, KD, 2])
        I("ada_w", [L, D, 3 * D]); I("ada_b", [L, 128, 3 * KD, 2]); I("norm_w", [L, 128, KD, 2])
        I("w_in", [L, D, cfg.DIN]); I("w_qb", [L, cfg.QR, cfg.MH * 192]); I("w_kvb", [L, cfg.KVR, cfg.MH * 256])
        I("w_out", [L, cfg.DMIX, D]); I("fin_w", [128, KD])
        I("qn_w", [L, 128, cfg.QR // 128]); I("kvn_w", [L, 128, cfg.KVR // 128])
        I("ret_nw", [L, 128, 1]); I("gdn_nw", [L, 128, 1])
        I("ret_ld", [L, 128, 2 * cfg.RH])
        I("gdn_al", [L, 128, 2 * cfg.GH]); I("gdn_dtb", [L, 128, 2 * cfg.GH])
        I("conv_w", [L, 128, 3 * cfg.GH, 5])
        I("rope_ret", [2, 128, T]); I("rope_mla", [2, 64, T])
        I("consts", [128, 12, 128])
        self.yT = nc.dram_tensor("yT", [D, cfg.SEQ], F32, kind="ExternalOutput").ap()
        S = self.scratch
        S("XT", [D, T], F32)
        S("hT", [D, T], BF16)
        S("OT", [cfg.DMIX, T], BF16)
        with contextlib.ExitStack() as es:
            self.es = es
            self.em = em = Em(nc, es)
            self.ps = [Tile(es.enter_context(nc.psum_tensor(f"ps{i}", [128, 512], F32)), psum=True) for i in range(6)]
            self.psA = Tile(es.enter_context(nc.psum_tensor("psA", [128, 512], F32)), psum=True)
            self.psB = Tile(es.enter_context(nc.psum_tensor("psB", [128, 512], F32)), psum=True)
            self.psi = 0
            self.setup_consts(es)
            xin = Tile(self.ins["xT"])
            for k in range(KD):
                em.dma("sync", self.scr["XT"][k * 128:(k + 1) * 128, :], self.ins["xT"][k * 128:(k + 1) * 128, :],
                       writes=[self.scr["XT"]])
            for l in range(L):
                self.layer(l)
            self.final_norm()
            em.barrier()
        return nc

    def setup_consts(self, es):
        em = self.em
        self.cst = sb(em, es, "cst", [128, 12, 128], F32)
        em.dma("sync", self.cst[:], self.ins["consts"][:], writes=[self.cst])
        self.ones_col = self.cst[:, 0, 0:1]
        self.ones_row = self.cst[0:1, 0, :]
        self.cc = sb(em, es, "cc", [128, self.cfg.KD, 2], F32)
        em.dma("sync", self.cc[:], self.ins["cc"][:], writes=[self.cc])
        self.scc = sb(em, es, "scc", [128, self.cfg.KD, 2], F32)
        em.op("scalar", lambda e: e.activation(out=self.scc[:], in_=self.cc[:], func=AF.Silu), [self.cc], [self.scc])
        self.finw = sb(em, es, "finw", [128, self.cfg.KD], F32)
        em.dma("sync", self.finw[:], self.ins["fin_w"][:], writes=[self.finw])

    def phase_mod(self, l, les):
        cfg, em = self.cfg, self.em
        KD, D = cfg.KD, cfg.D
        NO = 3 * KD
        self.mod = sb(em, les, f"mod{l}", [128, NO, 2], F32)
        self.g1 = sb(em, les, f"g1{l}", [128, KD, 2], F32)
        adab = sb(em, les, f"adab{l}", [128, NO, 2], F32)
        nw = sb(em, les, f"nw{l}", [128, KD, 2], F32)
        em.dma("sync", adab[:], self.ins["ada_b"][l], writes=[adab])
        em.dma("sync", nw[:], self.ins["norm_w"][l], writes=[nw])
        with contextlib.ExitStack() as es:
            CW = 512 if (3 * D) % 512 == 0 else 384
            wp = Pool(em, es, "adaw", [128, KD, CW], F32, 2)
            ps = self.psum()
            wv = self.ins["ada_w"][l].rearrange("(k p) c -> p k c", p=128)
            for c0 in range(0, 3 * D, CW):
                w = wp.get()
                for k in range(KD):
                    em.dma("sync", w[:, k, :], wv[:, k, c0:c0 + CW], writes=[w])
                for oc in range(c0 // 128, (c0 + CW) // 128):
                    r = oc * 128 - c0
                    for k in range(KD):
                        em.op("tensor", lambda e, k=k, r=r, oc=oc, w=w: e.matmul(
                            ps[:, 2 * oc:2 * oc + 2], lhsT=w[:, k, r:r + 128], rhs=self.scc[:, k, :],
                            start=(k == 0), stop=(k == KD - 1)), [w, self.scc], [ps])
            em.op("vector", lambda e: e.tensor_tensor(
                out=self.mod[:].rearrange("p a b -> p (a b)"), in0=ps[:, 0:2 * NO],
                in1=adab[:].rearrange("p a b -> p (a b)"), op=ALU.add), [ps, adab], [self.mod])
            em.op("vector", lambda e: e.scalar_tensor_tensor(
                out=self.g1[:], in0=self.mod[:, KD:2 * KD, :], scalar=1.0, in1=nw[:],
                op0=ALU.add, op1=ALU.mult), [self.mod, nw], [self.g1])
            em.barrier()

    def norm_fm(self, name, xv, KC, C, gfn, sfn, outv, out_tile, src_tile, eps, blocks, out_dtype=BF16,
                mean=True, extra_scale=1.0):
        cfg, em = self.cfg, self.em
        with contextlib.ExitStack() as es:
            xp = Pool(em, es, name + "x", [128, KC, 512], F32, 2)
            sqp = Pool(em, es, name + "sq", [128, 512], F32, 2)
            rp = Pool(em, es, name + "r", [1, 512], F32, 2)
            op_ = Pool(em, es, name + "o", [128, KC, 512], out_dtype, 2)
            tp = Pool(em, es, name + "t", [128, 512], F32, 2)
            for (t0, ts, isc) in blocks:
                x = xp.get()
                for k in range(KC):
                    em.dma("sync", x[:, k, :ts], xv[:, k, t0:t0 + ts], reads=[src_tile], writes=[x])
                pss = self.psum()
                for k in range(KC):
                    sq = sqp.get()
                    em.op("scalar", lambda e, sq=sq, k=k: e.activation(out=sq[:, :ts], in_=x[:, k, :ts], func=AF.Square),
                          [x], [sq])
                    em.op("tensor", lambda e, sq=sq, k=k: e.matmul(pss[0:1, :ts], lhsT=self.ones_col, rhs=sq[:, :ts],
                                                                     start=(k == 0), stop=(k == KC - 1)), [sq, self.cst], [pss])
                r = rp.get()
                em.op("vector", lambda e: e.tensor_scalar(out=r[:, :ts], in0=pss[0:1, :ts], scalar1=(1.0 / C if mean else 1.0),
                                                          scalar2=eps, op0=ALU.mult, op1=ALU.add), [pss], [r])
                em.op("scalar", lambda e: e.activation(out=r[:, :ts], in_=r[:, :ts], func=AF.Sqrt), [r], [r])
                em.op("vector", lambda e: e.reciprocal(out=r[:, :ts], in_=r[:, :ts]), [r], [r])
                psb = self.psum()
                em.op("tensor", lambda e: e.matmul(psb[:, :ts], lhsT=self.ones_row, rhs=r[:, :ts], start=True, stop=True),
                      [r, self.cst], [psb])
                o = op_.get()
                for k in range(KC):
                    tmp = tp.get()
                    em.op("vector", lambda e, k=k, tmp=tmp: e.tensor_tensor(out=tmp[:, :ts], in0=x[:, k, :ts], in1=psb[:, :ts],
                                                                           op=ALU.mult), [x, psb], [tmp])
                    g, gt = gfn(k, isc)
                    if sfn is not None:
                        s, st = sfn(k, isc)
                        em.op("scalar", lambda e, k=k, tmp=tmp, g=g, s=s: e.activation(
                            out=o[:, k, :ts], in_=tmp[:, :ts], func=AF.Identity, scale=g, bias=s), [tmp, gt, st], [o])
                    else:
                        em.op("scalar", lambda e, k=k, tmp=tmp, g=g: e.activation(
                            out=o[:, k, :ts], in_=tmp[:, :ts], func=AF.Copy, scale=g), [tmp, gt], [o])
                for k in range(KC):
                    em.dma("gpsimd", outv[:, k, t0:t0 + ts], o[:, k, :ts], reads=[o], writes=[out_tile])
            em.barrier()

    def gemm(self, name, Wd, K, actv, act_tile, jobs, blocks, slabw=1024):
        cfg, em = self.cfg, self.em
        KC = K // 128
        wv = Wd.rearrange("(k p) c -> p k c", p=128)
        slabs = []
        for j in jobs:
            if slabs and j["c0"] + j["cs"] - slabs[-1][0] <= slabw and j["c0"] >= slabs[-1][0]:
                slabs[-1][2].append(j)
                slabs[-1][1] = max(slabs[-1][1], j["c0"] + j["cs"])
            else:
                slabs.append([j["c0"], j["c0"] + j["cs"], [j]])
        with contextlib.ExitStack() as es:
            stg = Pool(em, es, name + "stg", [128, slabw], F32, 3)
            wsl = Pool(em, es, name + "w", [128, KC, slabw], BF16, 2)
            wrp = Pool(em, es, name + "wr", [128, KC, slabw], BF16, 1) if any(j.get("rope") for j in jobs) else None
            ap_ = Pool(em, es, name + "a", [128, KC, 512], BF16, 2)
            self.gemm_es = es
            for (s0, s1, sj) in slabs:
                sw = s1 - s0
                w = wsl.get()
                for k in range(KC):
                    st = stg.get()
                    em.dma("sync", st[:, :sw], wv[:, k, s0:s1], writes=[st])
                    eng = "gpsimd" if k % 2 == 0 else "vector"
                    em.op(eng, lambda e, st=st, k=k: e.tensor_copy(out=w[:, k, :sw], in_=st[:, :sw]), [st], [w])
                wr = None
                if any(j.get("rope") for j in sj):
                    wr = wrp.get()
                    for j in sj:
                        if not j.get("rope"):
                            continue
                        q = 32 if j["rope"] == "ret" else 16
                        r0 = j["c0"] - s0
                        for g in range(j["cs"] // (2 * q)):
                            a = r0 + g * 2 * q
                            em.op("vector", lambda e, a=a, q=q: e.tensor_scalar(
                                out=wr[:, :, a:a + q], in0=w[:, :, a + q:a + 2 * q], scalar1=-1.0, scalar2=None, op0=ALU.mult),
                                [w], [wr])
                            em.op("gpsimd", lambda e, a=a, q=q: e.tensor_copy(out=wr[:, :, a + q:a + 2 * q], in_=w[:, :, a:a + q]),
                                  [w], [wr])
                for blk in blocks:
                    (t0, ts, isc) = blk
                    a = ap_.get()
                    for k in range(KC):
                        em.dma("sync", a[:, k, :ts], actv[:, k, t0:t0 + ts], reads=[act_tile], writes=[a])
                    for j in sj:
                        r0, cs = j["c0"] - s0, j["cs"]
                        if j["kind"] == "fm":
                            ps = self.psum()
                            for k in range(KC):
                                em.op("tensor", lambda e, k=k, ps=ps: e.matmul(
                                    ps[:cs, :ts], lhsT=w[:, k, r0:r0 + cs], rhs=a[:, k, :ts], start=(k == 0), stop=(k == KC - 1)),
                                    [w, a], [ps])
                            ps2 = None
                            if j.get("rope"):
                                ps2 = self.psum()
                                for k in range(KC):
                                    em.op("tensor", lambda e, k=k, ps2=ps2: e.matmul(
                                        ps2[:cs, :ts], lhsT=wr[:, k, r0:r0 + cs], rhs=a[:, k, :ts], start=(k == 0),
                                        stop=(k == KC - 1)), [wr, a], [ps2])
                            j["epi"](j, blk, ps, ps2)
                        else:
                            for tt in range(0, ts, 128):
                                ps = self.psum()
                                for k in range(KC):
                                    em.op("tensor", lambda e, k=k, ps=ps, tt=tt: e.matmul(
                                        ps[:, :cs], lhsT=a[:, k, tt:tt + 128], rhs=w[:, k, r0:r0 + cs], start=(k == 0),
                                        stop=(k == KC - 1)), [w, a], [ps])
                                j["epi"](j, blk, ps, tt)
            em.barrier()

    def epi_pools(self, es):
        em = self.em
        self.e32 = Pool(em, es, "e32", [128, 512], F32, 3)
        self.e16 = Pool(em, es, "e16", [128, 512], BF16, 3)
        self.ert = Pool(em, es, "ert", [128, 2, 512], F32, 2)
        self.rt_cache = {}
        self.er = Pool(em, es, "er", [1, 512], F32, 2)
        self.eg = Pool(em, es, "eg", [128, 512], BF16, 2)

    def epi_copy32(self, dst, row0):
        def f(j, blk, ps, ps2):
            (t0, ts, isc), cs = blk, j["cs"]
            o = self.e32.get()
            self.em.op("scalar", lambda e: e.activation(out=o[:cs, :ts], in_=ps[:cs, :ts], func=AF.Copy), [ps], [o])
            self.em.dma("gpsimd", dst[row0(j):row0(j) + cs, t0:t0 + ts], o[:cs, :ts], reads=[o], writes=[dst])
        return f

    def epi_act16(self, dst, row0, func):
        def f(j, blk, ps, ps2):
            (t0, ts, isc), cs = blk, j["cs"]
            o = self.e16.get()
            self.em.op("scalar", lambda e: e.activation(out=o[:cs, :ts], in_=ps[:cs, :ts], func=func), [ps], [o])
            self.em.dma("gpsimd", dst[row0(j):row0(j) + cs, t0:t0 + ts], o[:cs, :ts], reads=[o], writes=[dst])
        return f

    def epi_rope16(self, dst, row0, tab):
        def f(j, blk, ps, ps2):
            (t0, ts, isc), cs = blk, j["cs"]
            em = self.em
            key = (tab, t0, id(self.gemm_es))
            if self.rt_cache.get("k") != key:
                rt = self.ert.get()
                n = 128 if tab == "rope_ret" else 64
                for i in range(2):
                    em.dma("sync", rt[:n, i, :ts], self.ins[tab][i, :, t0:t0 + ts], writes=[rt])
                self.rt_cache = {"k": key, "t": rt}
            rt = self.rt_cache["t"]
            t1, t2, o = self.e32.get(), self.e32.get(), self.e16.get()
            em.op("vector", lambda e: e.tensor_tensor(out=t1[:cs, :ts], in0=ps[:cs, :ts], in1=rt[:cs, 0, :ts], op=ALU.mult), [ps, rt], [t1])
            em.op("vector", lambda e: e.tensor_tensor(out=t2[:cs, :ts], in0=ps2[:cs, :ts], in1=rt[:cs, 1, :ts], op=ALU.mult), [ps2, rt], [t2])
            em.op("gpsimd", lambda e: e.tensor_tensor(out=o[:cs, :ts], in0=t1[:cs, :ts], in1=t2[:cs, :ts], op=ALU.add), [t1, t2], [o])
            em.dma("gpsimd", dst[row0(j):row0(j) + cs, t0:t0 + ts], o[:cs, :ts], reads=[o], writes=[dst])
        return f

    def epi_tm16(self, dst, col0):
        def f(j, blk, ps, tt):
            (t0, ts, isc), cs = blk, j["cs"]
            o = self.e16.get()
            self.em.op("scalar", lambda e: e.activation(out=o[:, :cs], in_=ps[:, :cs], func=AF.Copy), [ps], [o])
            self.em.dma("gpsimd", dst[t0 + tt:t0 + tt + 128, col0(j):col0(j) + cs], o[:, :cs], reads=[o], writes=[dst])
        return f

    def epi_tm32(self, dst, col0):
        def f(j, blk, ps, tt):
            (t0, ts, isc), cs = blk, j["cs"]
            o = self.e32.get()
            self.em.op("scalar", lambda e: e.activation(out=o[:, :cs], in_=ps[:, :cs], func=AF.Copy), [ps], [o])
            self.em.dma("gpsimd", dst[t0 + tt:t0 + tt + 128, col0(j):col0(j) + cs], o[:, :cs], reads=[o], writes=[dst])
        return f

    def layer(self, l):
        cfg, em = self.cfg, self.em
        D, T, KD = cfg.D, cfg.T, cfg.KD
        off = cfg.off
        S = self.scr
        with contextlib.ExitStack() as les:
            self.phase_mod(l, les)
            KDm = KD
            xv = S["XT"][:].rearrange("(k p) t -> p k t", p=128)
            hv = S["hT"][:].rearrange("(k p) t -> p k t", p=128)
            self.norm_fm("n1", xv, KD, D, lambda k, c: (self.g1[:, k, c:c + 1], self.g1),
                         lambda k, c: (self.mod[:, k, c:c + 1], self.mod), hv, S["hT"], S["XT"], EPS, cfg.blocks)
            self.proj_in(l)
            self.mla(l)
            self.retention(l)
            self.gdn(l)
            self.proj_out(l)

    def get_scr(self, name, shape, dtype):
        if name not in self.scr:
            self.scratch(name, shape, dtype)
        return self.scr[name]

    def proj_in(self, l):
        cfg, em = self.cfg, self.em
        T, off = cfg.T, cfg.off
        G = self.get_scr
        rqT, rkT = G("rqT", [cfg.DRET, T], BF16), G("rkT", [cfg.DRET, T], BF16)
        rv, rgT = G("rv", [T, cfg.DRET], BF16), G("rgT", [cfg.DRET, T], BF16)
        gqkvT = G("gqkvT", [3 * cfg.DGDN, T], F32)
        ggT = G("ggT", [cfg.DGDN, T], BF16)
        gab = G("gab", [T, 4 * cfg.GH], F32)
        cqT, ckvT = G("cqT", [cfg.QR, T], F32), G("ckvT", [cfg.KVR, T], F32)
        krT, mgT = G("krT", [64, T], BF16), G("mgT", [cfg.DMLA, T], BF16)
        jobs = []

        def add(c0, n, step, kind, epi, rope=None):
            for c in range(c0, c0 + n, step):
                jobs.append(dict(c0=c, cs=min(step, c0 + n - c), kind=kind, epi=epi, rope=rope))
        with contextlib.ExitStack() as es:
            self.epi_pools(es)
            add(off[0], cfg.DRET, 128, "fm", self.epi_rope16(rqT, lambda j: j["c0"] - off[0], "rope_ret"), "ret")
            add(off[1], cfg.DRET, 128, "fm", self.epi_rope16(rkT, lambda j: j["c0"] - off[1], "rope_ret"), "ret")
            add(off[2], cfg.DRET, 512, "tm", self.epi_tm16(rv, lambda j: j["c0"] - off[2]))
            add(off[3], cfg.DRET, 128, "fm", self.epi_act16(rgT, lambda j: j["c0"] - off[3], AF.Silu))
            add(off[4], 3 * cfg.DGDN, 128, "fm", self.epi_copy32(gqkvT, lambda j: j["c0"] - off[4]))
            add(off[7], cfg.DGDN, 128, "fm", self.epi_act16(ggT, lambda j: j["c0"] - off[7], AF.Silu))
            add(off[8], 4 * cfg.GH, 128, "tm", self.epi_tm32(gab, lambda j: j["c0"] - off[8]))
            add(off[10], cfg.QR, 128, "fm", self.epi_copy32(cqT, lambda j: j["c0"] - off[10]))
            add(off[11], cfg.KVR, 128, "fm", self.epi_copy32(ckvT, lambda j: j["c0"] - off[11]))
            add(off[12], 64, 64, "fm", self.epi_rope16(krT, lambda j: 0, "rope_mla"), "mla")
            add(off[13], cfg.DMLA, 128, "fm", self.epi_act16(mgT, lambda j: j["c0"] - off[13], AF.Silu))
            hv = self.scr["hT"][:].rearrange("(k p) t -> p k t", p=128)
            self.gemm("gin", self.ins["w_in"][l], cfg.D, hv, self.scr["hT"], jobs, cfg.blocks)

    def proj_out(self, l):
        cfg, em = self.cfg, self.em
        XT = self.scr["XT"]
        last = (l == cfg.DEPTH - 1)
        with contextlib.ExitStack() as es:
            self.epi_pools(es)
            xp = Pool(em, es, "pox", [128, 512], F32, 3)

            def epi(j, blk, ps, ps2):
                (t0, ts, isc), cs = blk, j["cs"]
                k = j["c0"] // 128
                x = xp.get()
                em.dma("sync", x[:, :ts], XT[k * 128:(k + 1) * 128, t0:t0 + ts], reads=[XT], writes=[x])
                o = self.e32.get()
                em.op("vector", lambda e: e.scalar_tensor_tensor(out=o[:, :ts], in0=ps[:, :ts], scalar=self.mod[:, 2 * cfg.KD + k, isc:isc + 1],
                                                                  in1=x[:, :ts], op0=ALU.mult, op1=ALU.add), [ps, x, self.mod], [o])
                em.dma("gpsimd", XT[k * 128:(k + 1) * 128, t0:t0 + ts], o[:, :ts], reads=[o], writes=[XT])
            jobs = [dict(c0=c, cs=128, kind="fm", epi=epi) for c in range(0, cfg.D, 128)]
            ov = self.scr["OT"][:].rearrange("(k p) t -> p k t", p=128)
            blocks = [b for b in cfg.blocks if not (last and b[2])]
            self.gemm("gout", self.ins["w_out"][l], cfg.DMIX, ov, self.scr["OT"], jobs, blocks)

    def final_norm(self):
        cfg = self.cfg
        xv = self.scr["XT"][:].rearrange("(k p) t -> p k t", p=128)
        yt = Tile(self.yT)
        class Sh:
            def __init__(s, ap, o): s.ap, s.o = ap, o
            def __getitem__(s, key):
                p, k, tsl = key
                return s.ap[p, k, tsl.start - s.o:tsl.stop - s.o]
        yv = Sh(self.yT.rearrange("(k p) t -> p k t", p=128), cfg.CTX)
        self.norm_fm("nf", xv, cfg.KD, cfg.D, lambda k, c: (self.finw[:, k:k + 1], self.finw), None, yv, yt, self.scr["XT"], EPS,
                     [b for b in cfg.blocks if not b[2]], out_dtype=F32)
        self.em.final_wait(yt)

    def head_norm(self, ps, ts, nw, gate, grow, orow, t0, src=None):
        em = self.em
        OT = self.scr["OT"]
        osb, sq = self.e32.get(), self.e32.get()
        if src is None:
            src = ps[:, :ts]
        em.op("scalar", lambda e: e.activation(out=osb[:, :ts], in_=src, func=AF.Copy), [ps], [osb])
        em.op("scalar", lambda e: e.activation(out=sq[:, :ts], in_=src, func=AF.Square), [ps], [sq])
        pss = self.psum()
        em.op("tensor", lambda e: e.matmul(pss[0:1, :ts], lhsT=self.ones_col, rhs=sq[:, :ts], start=True, stop=True), [sq, self.cst], [pss])
        r = self.er.get()
        em.op("vector", lambda e: e.tensor_scalar(out=r[:, :ts], in0=pss[0:1, :ts], scalar1=1.0 / 128, scalar2=EPS, op0=ALU.mult, op1=ALU.add), [pss], [r])
        em.op("scalar", lambda e: e.activation(out=r[:, :ts], in_=r[:, :ts], func=AF.Sqrt), [r], [r])
        em.op("vector", lambda e: e.reciprocal(out=r[:, :ts], in_=r[:, :ts]), [r], [r])
        psb = self.psum()
        em.op("tensor", lambda e: e.matmul(psb[:, :ts], lhsT=self.ones_row, rhs=r[:, :ts], start=True, stop=True), [r, self.cst], [psb])
        g = self.eg.get()
        em.dma("sync", g[:, :ts], gate[grow:grow + 128, t0:t0 + ts], reads=[gate], writes=[g])
        t1, o = self.e32.get(), self.e16.get()
        em.op("vector", lambda e: e.scalar_tensor_tensor(out=t1[:, :ts], in0=osb[:, :ts], scalar=nw[:, 0:1], in1=psb[:, :ts], op0=ALU.mult, op1=ALU.mult), [osb, nw, psb], [t1])
        em.op("gpsimd", lambda e: e.tensor_tensor(out=o[:, :ts], in0=t1[:, :ts], in1=g[:, :ts], op=ALU.mult), [t1, g], [o])
        em.dma("gpsimd", OT[orow:orow + 128, t0:t0 + ts], o[:, :ts], reads=[o], writes=[OT])

    def mla(self, l):
        cfg, em = self.cfg, self.em
        T, MH, NCH, NCC = cfg.T, cfg.MH, cfg.NCH, cfg.NCC
        G = self.get_scr
        last = (l == cfg.DEPTH - 1)
        cqnT, ckvnT = G("cqnT", [cfg.QR, T], BF16), G("ckvnT", [cfg.KVR, T], BF16)
        mqT, mkT, mv = G("mqT", [MH * 192, T], BF16), G("mkT", [MH * 128, T], BF16), G("mv", [T, MH * 128], BF16)
        S = self.scr
        with contextlib.ExitStack() as es:
            qnw = sb(em, es, "qnw", [128, cfg.QR // 128], F32)
            kvnw = sb(em, es, "kvnw", [128, cfg.KVR // 128], F32)
            em.dma("sync", qnw[:], self.ins["qn_w"][l], writes=[qnw])
            em.dma("sync", kvnw[:], self.ins["kvn_w"][l], writes=[kvnw])
            v3 = lambda t: t[:].rearrange("(k p) t -> p k t", p=128)
            self.norm_fm("nq", v3(S["cqT"]), cfg.QR // 128, cfg.QR, lambda k, c: (qnw[:, k:k + 1], qnw), None, v3(cqnT), cqnT, S["cqT"], EPS, cfg.blocks)
            self.norm_fm("nkv", v3(S["ckvT"]), cfg.KVR // 128, cfg.KVR, lambda k, c: (kvnw[:, k:k + 1], kvnw), None, v3(ckvnT), ckvnT, S["ckvT"], EPS, cfg.blocks)
        with contextlib.ExitStack() as es:
            self.epi_pools(es)
            jobs = []
            for h in range(MH):
                jobs.append(dict(c0=h * 192, cs=128, kind="fm", epi=self.epi_act16(mqT, lambda j: j["c0"], AF.Copy)))
                jobs.append(dict(c0=h * 192 + 128, cs=64, kind="fm", rope="mla", epi=self.epi_rope16(mqT, lambda j: j["c0"], "rope_mla")))
            self.gemm("gqb", self.ins["w_qb"][l], cfg.QR, cqnT[:].rearrange("(k p) t -> p k t", p=128), cqnT, jobs, cfg.blocks, slabw=960)
        with contextlib.ExitStack() as es:
            self.epi_pools(es)
            jobs = []
            for h in range(MH):
                jobs.append(dict(c0=h * 256, cs=128, kind="fm", epi=self.epi_act16(mkT, lambda j: j["c0"] // 2, AF.Copy)))
                jobs.append(dict(c0=h * 256 + 128, cs=128, kind="tm", epi=self.epi_tm16(mv, lambda j: (j["c0"] - 128) // 2)))
            self.gemm("gkvb", self.ins["w_kvb"][l], cfg.KVR, ckvnT[:].rearrange("(k p) t -> p k t", p=128), ckvnT, jobs, cfg.blocks)
        OT, mgT, krT = S["OT"], S["mgT"], S["krT"]
        scale = 192.0 ** -0.5
        with contextlib.ExitStack() as es:
            self.epi_pools(es)
            kr = sb(em, es, "kr", [64, T], BF16)
            em.dma("sync", kr[:], krT[:, :], reads=[krT], writes=[kr])
            ones16 = sb(em, es, "ones16", [128, 1], BF16)
            em.op("vector", lambda e: e.tensor_copy(out=ones16[:], in_=self.cst[:, 0, 0:1]), [self.cst], [ones16])
            knp = Pool(em, es, "kn", [128, T], BF16, 2)
            vp = Pool(em, es, "mvv", [128, NCH, 128], BF16, 2)
            qp = Pool(em, es, "mq", [128, 2, 512], BF16, 2)
            pp = Pool(em, es, "mp", [128, 512], BF16, 3)
            rbp = Pool(em, es, "mrb", [128, 512], F32, 2)
            psA, psB = self.psA, self.psB
            for h in range(MH):
                kn, v = knp.get(), vp.get()
                em.dma("sync", kn[:], mkT[h * 128:(h + 1) * 128, :], reads=[mkT], writes=[kn])
                em.dma("sync", v[:], mv[:, h * 128:(h + 1) * 128].rearrange("(c p) e -> p c e", p=128), reads=[mv], writes=[v])
                for (t0, ts, isc) in cfg.blocks:
                    if isc and last:
                        continue
                    nk = NCC if isc else NCH
                    q = qp.get()
                    em.dma("sync", q[:, 0, :ts], mqT[h * 192:h * 192 + 128, t0:t0 + ts], reads=[mqT], writes=[q])
                    em.dma("sync", q[0:64, 1, :ts], mqT[h * 192 + 128:h * 192 + 192, t0:t0 + ts], reads=[mqT], writes=[q])
                    for kt in range(nk):
                        pss = self.psum()
                        ks = slice(kt * 128, (kt + 1) * 128)
                        em.op("tensor", lambda e: e.matmul(pss[:, :ts], lhsT=kn[:, ks], rhs=q[:, 0, :ts], start=True, stop=False), [kn, q], [pss])
                        em.op("tensor", lambda e: e.matmul(pss[:, :ts], lhsT=kr[0:64, ks], rhs=q[0:64, 1, :ts], start=False, stop=True), [kr, q], [pss])
                        p = pp.get()
                        em.op("scalar", lambda e: e.activation(out=p[:, :ts], in_=pss[:, :ts], func=AF.Exp, scale=scale), [pss], [p])
                        em.op("tensor", lambda e: e.matmul(psA[:, :ts], lhsT=v[:, kt, :], rhs=p[:, :ts], start=(kt == 0), stop=(kt == nk - 1)), [v, p], [psA])
                        em.op("tensor", lambda e: e.matmul(psB[0:1, :ts], lhsT=ones16[:], rhs=p[:, :ts], start=(kt == 0), stop=(kt == nk - 1)), [ones16, p], [psB])
                    r = self.er.get()
                    em.op("vector", lambda e: e.reciprocal(out=r[:, :ts], in_=psB[0:1, :ts]), [psB], [r])
                    psb = self.psum()
                    em.op("tensor", lambda e: e.matmul(psb[:, :ts], lhsT=self.ones_row, rhs=r[:, :ts], start=True, stop=True), [r, self.cst], [psb])
                    rb = rbp.get()
                    em.op("scalar", lambda e: e.activation(out=rb[:, :ts], in_=psb[:, :ts], func=AF.Copy), [psb], [rb])
                    g = self.eg.get()
                    em.dma("sync", g[:, :ts], mgT[h * 128:(h + 1) * 128, t0:t0 + ts], reads=[mgT], writes=[g])
                    t1, o = self.e32.get(), self.e16.get()
                    em.op("vector", lambda e: e.tensor_tensor(out=t1[:, :ts], in0=psA[:, :ts], in1=rb[:, :ts], op=ALU.mult), [psA, rb], [t1])
                    em.op("gpsimd", lambda e: e.tensor_tensor(out=o[:, :ts], in0=t1[:, :ts], in1=g[:, :ts], op=ALU.mult), [t1, g], [o])
                    ro = cfg.DRET + cfg.DGDN + h * 128
                    em.dma("gpsimd", OT[ro:ro + 128, t0:t0 + ts], o[:, :ts], reads=[o], writes=[OT])
            em.barrier()

    def retention(self, l):
        cfg, em = self.cfg, self.em
        T, RH, NCH, NCC = cfg.T, cfg.RH, cfg.NCH, cfg.NCC
        S = self.scr
        rqT, rkT, rv, rgT = S["rqT"], S["rkT"], S["rv"], S["rgT"]
        sk = 128.0 ** -0.5
        order = [list(range(NCH)), list(range(NCC - 1, -1, -1)) + list(range(NCH - 1, NCC - 1, -1))]
        with contextlib.ExitStack() as es:
            self.epi_pools(es)
            ld = sb(em, es, "rld", [128, 2 * RH], F32)
            nw = sb(em, es, "rnw", [128, 1], F32)
            em.dma("sync", ld[:], self.ins["ret_ld"][l], writes=[ld])
            em.dma("sync", nw[:], self.ins["ret_nw"][l], writes=[nw])
            id16 = sb(em, es, "id16", [128, 128], BF16)
            em.op("vector", lambda e: e.tensor_copy(out=id16[:], in_=self.cst[:, 8, :]), [self.cst], [id16])
            ex = sb(em, es, "rex", [128, 2, 128], F32)
            DT = sb(em, es, "rDT", [128, 128], F32)
            QS = sb(em, es, "rQS", [128, 2, 128], F32)
            KS = sb(em, es, "rKS", [128, 2], F32)
            aa = sb(em, es, "raa", [128, 2], F32)
            qT = sb(em, es, "rq", [128, NCH, 128], BF16)
            kT = sb(em, es, "rk", [128, NCH, 128], BF16)
            v = sb(em, es, "rvv", [128, NCH, 128], BF16)
            kd = [sb(em, es, f"rkd{d}", [128, NCH, 128], BF16) for d in range(2)]
            qs = [sb(em, es, f"rqs{d}", [128, NCH, 128], BF16) for d in range(2)]
            Sin = [sb(em, es, f"rSin{d}", [128, NCH, 128], BF16) for d in range(2)]
            Sp = Pool(em, es, "rS", [128, 128], F32, 3)
            atp = Pool(em, es, "rat", [128, 128], BF16, 3)
            cst = self.cst
            for h in range(RH):
                lam = [ld[:, h:h + 1], ld[:, RH + h:RH + h + 1]]
                for d in range(2):
                    em.op("scalar", lambda e: e.activation(out=ex[:, d, :], in_=cst[:, 1 + 2 * d, :], func=AF.Exp, scale=lam[d]), [cst, ld], [ex])
                    em.op("vector", lambda e: e.scalar_tensor_tensor(out=ex[:, d, :], in0=ex[:, d, :], scalar=sk, in1=cst[:, 2 + 2 * d, :], op0=ALU.mult, op1=ALU.mult), [ex, cst], [ex])
                    em.op("scalar", lambda e: e.activation(out=QS[:, d, :], in_=cst[:, 5 + d, :], func=AF.Exp, scale=lam[d]), [cst, ld], [QS])
                    em.op("scalar", lambda e: e.activation(out=KS[:, d:d + 1], in_=cst[:, 7, d:d + 1], func=AF.Exp, scale=lam[d]), [cst, ld], [KS])
                    em.op("scalar", lambda e: e.activation(out=aa[:, d:d + 1], in_=cst[:, 7, 2:3], func=AF.Exp, scale=lam[d]), [cst, ld], [aa])
                em.op("vector", lambda e: e.tensor_tensor(out=DT[:], in0=ex[:, 0, :], in1=ex[:, 1, :], op=ALU.add), [ex], [DT])
                em.op("vector", lambda e: e.tensor_scalar(out=KS[:], in0=KS[:], scalar1=sk, scalar2=None, op0=ALU.mult), [KS], [KS])
                if STOP.get("ret") == 1:
                    break
                hs = slice(h * 128, (h + 1) * 128)
                em.dma("sync", qT[:].rearrange("p c e -> p (c e)"), rqT[hs, :], reads=[rqT], writes=[qT])
                em.dma("sync", kT[:].rearrange("p c e -> p (c e)"), rkT[hs, :], reads=[rkT], writes=[kT])
                em.dma("sync", v[:], rv[:, hs].rearrange("(c p) e -> p c e", p=128), reads=[rv], writes=[v])
                if STOP.get("ret") == 21:
                    break
                for j0 in range(0, NCH, 4):
                    n = min(4, NCH - j0)
                    pt = self.psum()
                    for j in range(j0, j0 + n):
                        em.op("tensor", lambda e: e.matmul(pt[:, (j - j0) * 128:(j - j0 + 1) * 128], lhsT=kT[:, j, :], rhs=id16[:], start=True, stop=True), [kT, id16], [pt])
                    if STOP.get("ret") == 22:
                        continue
                    em.op("vector", lambda e: e.tensor_scalar(
                        out=kd[0][:, j0:j0 + n, :].rearrange("p c e -> p (c e)"), in0=pt[:, :n * 128], scalar1=KS[:, 0:1], scalar2=None, op0=ALU.mult), [pt, KS], [kd[0]])
                    if STOP.get("ret") == 23:
                        continue
                    em.op("scalar", lambda e: e.activation(out=kd[1][:, j0:j0 + n, :].rearrange("p c e -> p (c e)"), in_=pt[:, :n * 128], func=AF.Copy, scale=KS[:, 1:2]),
                          [pt, KS], [kd[1]])
                if STOP.get("ret") in (2, 22, 23):
                    break
                for d in range(2):
                    em.op("vector" if d == 0 else "gpsimd", lambda e: e.tensor_tensor(
                        out=qs[d][:], in0=qT[:], in1=QS[:, d:d + 1, :].broadcast_to([128, NCH, 128]), op=ALU.mult), [qT, QS], [qs[d]])
                if STOP.get("ret") == 3:
                    break
                for d in range(2):
                    Sc = None
                    for idx, j in enumerate(order[d]):
                        if idx == 0:
                            em.op("gpsimd", lambda e: e.memset(Sin[d][:, j, :], 0.0), [], [Sin[d]])
                        else:
                            em.op("scalar", lambda e: e.activation(out=Sin[d][:, j, :], in_=Sc[:], func=AF.Copy), [Sc], [Sin[d]])
                        if idx == NCH - 1:
                            break
                        U = self.psum()
                        em.op("tensor", lambda e: e.matmul(U[:, 0:128], lhsT=kd[d][:, j, :], rhs=v[:, j, :], start=True, stop=True), [kd[d], v], [U])
                        S2 = Sp.get()
                        if idx == 0:
                            em.op("vector", lambda e: e.tensor_copy(out=S2[:], in_=U[:, 0:128]), [U], [S2])
                        else:
                            em.op("vector", lambda e: e.scalar_tensor_tensor(out=S2[:], in0=Sc[:], scalar=aa[:, d:d + 1], in1=U[:, 0:128], op0=ALU.mult, op1=ALU.add), [Sc, aa, U], [S2])
                        Sc = S2
                if STOP.get("ret") == 4:
                    break
                psA = self.psA
                for (t0, ts, isc) in cfg.blocks:
                    for ci in range(ts // 128):
                        j = t0 // 128 + ci
                        psa = self.psum()
                        em.op("tensor", lambda e: e.matmul(psa[:, 0:128], lhsT=kT[:, j, :], rhs=qT[:, j, :], start=True, stop=True), [kT, qT], [psa])
                        at = atp.get()
                        em.op("vector", lambda e: e.tensor_tensor(out=at[:], in0=psa[:, 0:128], in1=DT[:], op=ALU.mult), [psa, DT], [at])
                        cs_ = slice(ci * 128, (ci + 1) * 128)
                        em.op("tensor", lambda e: e.matmul(psA[:, cs_], lhsT=v[:, j, :], rhs=at[:], start=True, stop=False), [v, at], [psA])
                        em.op("tensor", lambda e: e.matmul(psA[:, cs_], lhsT=Sin[0][:, j, :], rhs=qs[0][:, j, :], start=False, stop=False), [Sin[0], qs[0]], [psA])
                        em.op("tensor", lambda e: e.matmul(psA[:, cs_], lhsT=Sin[1][:, j, :], rhs=qs[1][:, j, :], start=False, stop=True), [Sin[1], qs[1]], [psA])
                    self.head_norm(psA, ts, nw, rgT, h * 128, h * 128, t0)
            em.barrier()

    def gdn(self, l):
        cfg, em = self.cfg, self.em
        T, GH, NCH, NCC = cfg.T, cfg.GH, cfg.NCH, cfg.NCC
        S = self.scr
        gqkvT, ggT, gab = S["gqkvT"], S["ggT"], S["gab"]
        order = [list(range(NCH)), list(range(NCC - 1, -1, -1)) + list(range(NCH - 1, NCC - 1, -1))]
        cst = self.cst
        ident = cst[:, 8, :]
        op = em.op
        with contextlib.ExitStack() as es:
            self.epi_pools(es)
            cw = sb(em, es, "gcw", [128, 3 * GH, 5], F32)
            nw = sb(em, es, "gnw", [128, 1], F32)
            al = sb(em, es, "gal", [128, 2 * GH], F32)
            dtb = sb(em, es, "gdtb", [128, 2 * GH], F32)
            em.dma("sync", cw[:], self.ins["conv_w"][l], writes=[cw])
            em.dma("sync", nw[:], self.ins["gdn_nw"][l], writes=[nw])
            em.dma("sync", al[:], self.ins["gdn_al"][l], writes=[al])
            em.dma("sync", dtb[:], self.ins["gdn_dtb"][l], writes=[dtb])
            id16 = sb(em, es, "gid16", [128, 128], BF16)
            op("vector", lambda e: e.tensor_copy(out=id16[:], in_=ident), [cst], [id16])
            op("scalar", lambda e: e.activation(out=al[:], in_=al[:], func=AF.Exp), [al], [al])
            op("vector", lambda e: e.tensor_scalar(out=al[:], in0=al[:], scalar1=-1.0, scalar2=None, op0=ALU.mult), [al], [al])
            ab = sb(em, es, "gab", [128, NCH, 4 * GH], F32)
            em.dma("sync", ab[:], gab[:, :].rearrange("(c p) r -> p c r", p=128), reads=[gab], writes=[ab])
            g_tm = sb(em, es, "ggtm", [128, NCH, 2 * GH], F32)
            beta = sb(em, es, "gbeta", [128, NCH, 2 * GH], F32)
            for dh in range(2 * GH):
                op("scalar", lambda e: e.activation(out=g_tm[:, :, dh], in_=ab[:, :, dh], func=AF.Exp, bias=dtb[:, dh:dh + 1]), [ab, dtb], [g_tm])
            op("scalar", lambda e: e.activation(out=g_tm[:], in_=g_tm[:], func=AF.Ln, bias=1.0), [g_tm], [g_tm])
            for dh in range(2 * GH):
                op("vector", lambda e: e.tensor_scalar(out=g_tm[:, :, dh], in0=g_tm[:, :, dh], scalar1=al[:, dh:dh + 1], scalar2=None, op0=ALU.mult), [g_tm, al], [g_tm])
            op("scalar", lambda e: e.activation(out=beta[:], in_=ab[:, :, 2 * GH:4 * GH], func=AF.Sigmoid), [ab], [beta])
            gc, ngc, bk, kds, egl = [], [], [], [], []
            for d in range(2):
                gsl = g_tm[:, :, d * GH:(d + 1) * GH]
                t_gc = sb(em, es, f"ggc{d}", [128, NCH, GH], F32)
                t_gl = sb(em, es, f"ggl{d}", [128, NCH, GH], F32)
                ps = self.psum()
                op("tensor", lambda e: e.matmul(ps[:, :NCH * GH].rearrange("p (c h) -> p c h", h=GH), lhsT=cst[:, 2 + 2 * d, :], rhs=gsl, start=True, stop=True), [cst, g_tm], [ps])
                op("vector", lambda e: e.tensor_copy(out=t_gc[:].rearrange("p c h -> p (c h)"), in_=ps[:, :NCH * GH]), [ps], [t_gc])
                ps = self.psum()
                op("tensor", lambda e: e.matmul(ps[:, :NCH * GH].rearrange("p (c h) -> p c h", h=GH), lhsT=cst[:, 0, :], rhs=gsl, start=True, stop=True), [cst, g_tm], [ps])
                op("vector", lambda e: e.tensor_copy(out=t_gl[:].rearrange("p c h -> p (c h)"), in_=ps[:, :NCH * GH]), [ps], [t_gl])
                t_ngc = sb(em, es, f"gngc{d}", [128, NCH, GH], F32)
                t_bk = sb(em, es, f"gbk{d}", [128, NCH, GH], F32)
                t_kds = sb(em, es, f"gkds{d}", [128, NCH, GH], F32)
                t_egl = sb(em, es, f"gegl{d}", [128, NCH, GH], F32)
                op("vector", lambda e: e.tensor_scalar(out=t_ngc[:], in0=t_gc[:], scalar1=-1.0, scalar2=None, op0=ALU.mult), [t_gc], [t_ngc])
                op("scalar", lambda e: e.activation(out=t_bk[:], in_=t_gc[:], func=AF.Exp), [t_gc], [t_bk])
                op("vector", lambda e: e.tensor_tensor(out=t_bk[:], in0=t_bk[:], in1=beta[:, :, d * GH:(d + 1) * GH], op=ALU.mult), [t_bk, beta], [t_bk])
                op("vector", lambda e: e.tensor_tensor(out=t_kds[:], in0=t_gl[:], in1=t_gc[:], op=ALU.subtract), [t_gl, t_gc], [t_kds])
                op("scalar", lambda e: e.activation(out=t_kds[:], in_=t_kds[:], func=AF.Exp), [t_kds], [t_kds])
                op("scalar", lambda e: e.activation(out=t_egl[:], in_=t_gl[:], func=AF.Exp), [t_gl], [t_egl])
                gc.append(t_gc); ngc.append(t_ngc); bk.append(t_bk); kds.append(t_kds); egl.append(t_egl)
            qT = sb(em, es, "gq", [128, T], BF16)
            kT = sb(em, es, "gk", [128, T], BF16)
            vT = sb(em, es, "gv", [128, T], BF16)
            k_tm = sb(em, es, "gktm", [128, NCH, 128], BF16)
            v_tm = sb(em, es, "gvtm", [128, NCH, 128], BF16)
            Oacc = sb(em, es, "gO", [128, T], F32)
            xtp = Pool(em, es, "gxt", [128, 516], F32, 2)
            accp = Pool(em, es, "gacc", [128, 512], F32, 2)
            m32 = Pool(em, es, "gm32", [128, 128], F32, 20)
            m16 = Pool(em, es, "gm16", [128, 128], BF16, 16)
            r1p = Pool(em, es, "gr1", [128, 256], F32, 2)
            Sp = Pool(em, es, "gS", [128, 128], F32, 3)
            Sbp = Pool(em, es, "gSb", [128, 128], BF16, 3)
            for h in range(GH):
                for which, dst in ((0, qT), (1, kT), (2, vT)):
                    ch = which * GH + h
                    for (t0, ts, isc) in cfg.blocks:
                        seg0, seg1 = (0, cfg.CTX) if isc else (cfg.CTX, T)
                        lo, hi = max(seg0, t0 - 2), min(seg1, t0 + ts + 2)
                        xt = xtp.get()
                        op("gpsimd", lambda e: e.memset(xt[:], 0.0), [], [xt])
                        em.dma("sync", xt[:, lo - (t0 - 2):hi - (t0 - 2)], gqkvT[ch * 128:(ch + 1) * 128, lo:hi], reads=[gqkvT], writes=[xt])
                        acc = accp.get()
                        op("vector", lambda e: e.tensor_scalar(out=acc[:, :ts], in0=xt[:, 0:ts], scalar1=cw[:, ch, 0:1], scalar2=None, op0=ALU.mult), [xt, cw], [acc])
                        for j in range(1, 5):
                            op("vector", lambda e: e.scalar_tensor_tensor(out=acc[:, :ts], in0=xt[:, j:j + ts], scalar=cw[:, ch, j:j + 1], in1=acc[:, :ts], op0=ALU.mult, op1=ALU.add), [xt, cw, acc], [acc])
                        if which == 2:
                            op("scalar", lambda e: e.activation(out=dst[:, t0:t0 + ts], in_=acc[:, :ts], func=AF.Silu), [acc], [dst])
                            continue
                        sl, sq = self.e32.get(), self.e32.get()
                        op("scalar", lambda e: e.activation(out=sl[:, :ts], in_=acc[:, :ts], func=AF.Silu), [acc], [sl])
                        op("scalar", lambda e: e.activation(out=sq[:, :ts], in_=sl[:, :ts], func=AF.Square), [sl], [sq])
                        pss = self.psum()
                        op("tensor", lambda e: e.matmul(pss[0:1, :ts], lhsT=self.ones_col, rhs=sq[:, :ts], start=True, stop=True), [sq, cst], [pss])
                        r = self.er.get()
                        op("vector", lambda e: e.tensor_scalar(out=r[:, :ts], in0=pss[0:1, :ts], scalar1=1.0, scalar2=EPS, op0=ALU.mult, op1=ALU.add), [pss], [r])
                        op("scalar", lambda e: e.activation(out=r[:, :ts], in_=r[:, :ts], func=AF.Sqrt), [r], [r])
                        op("vector", lambda e: e.reciprocal(out=r[:, :ts], in_=r[:, :ts]), [r], [r])
                        psb = self.psum()
                        op("tensor", lambda e: e.matmul(psb[:, :ts], lhsT=self.ones_row, rhs=r[:, :ts], start=True, stop=True), [r, cst], [psb])
                        op("vector", lambda e: e.scalar_tensor_tensor(out=dst[:, t0:t0 + ts], in0=sl[:, :ts], scalar=(128.0 ** -0.5 if which == 0 else 1.0), in1=psb[:, :ts], op0=ALU.mult, op1=ALU.mult), [sl, psb], [dst])
                for src, dstt in ((kT, k_tm), (vT, v_tm)):
                    for j0 in range(0, NCH, 4):
                        n = min(4, NCH - j0)
                        pt = self.psum()
                        for j in range(j0, j0 + n):
                            op("tensor", lambda e: e.matmul(pt[:, (j - j0) * 128:(j - j0 + 1) * 128], lhsT=src[:, j * 128:(j + 1) * 128], rhs=id16[:], start=True, stop=True), [src, id16], [pt])
                        op("scalar", lambda e: e.activation(out=dstt[:, j0:j0 + n, :].rearrange("p c e -> p (c e)"), in_=pt[:, :n * 128], func=AF.Copy), [pt], [dstt])
                for d in range(2):
                    S32, Sb = Sp.get(), Sbp.get()
                    op("vector", lambda e: e.memset(S32[:], 0.0), [], [S32])
                    op("gpsimd", lambda e: e.memset(Sb[:], 0.0), [], [Sb])
                    Ltri, Minc, Mstr = cst[:, 2 + 2 * d, :], cst[:, 2 + 2 * d, :], cst[:, 9 + d, :]
                    for j in order[d]:
                        js = slice(j * 128, (j + 1) * 128)
                        col = lambda t: t[:, j, h:h + 1]
                        R1 = r1p.get()
                        op("gpsimd", lambda e: e.tensor_scalar(out=R1[:, 0:128], in0=Ltri, scalar1=g_tm[:, j, d * GH + h:d * GH + h + 1], scalar2=None, op0=ALU.mult), [cst, g_tm], [R1])
                        op("gpsimd", lambda e: e.tensor_scalar(out=R1[:, 128:256], in0=ident, scalar1=beta[:, j, d * GH + h:d * GH + h + 1], scalar2=None, op0=ALU.mult), [cst, beta], [R1])
                        psg = self.psum()
                        op("tensor", lambda e: e.matmul(psg[:, 0:256], lhsT=cst[:, 0, :], rhs=R1[:], start=True, stop=True), [cst, R1], [psg])
                        E = m32.get()
                        op("vector", lambda e: e.tensor_scalar(out=E[:], in0=psg[:, 0:128], scalar1=col(ngc[d]), scalar2=0.0, op0=ALU.add, op1=ALU.min), [psg, ngc[d]], [E])
                        op("scalar", lambda e: e.activation(out=E[:], in_=E[:], func=AF.Exp), [E], [E])
                        DTi, BBs, EB = m32.get(), m32.get(), m32.get()
                        op("gpsimd", lambda e: e.tensor_tensor(out=DTi[:], in0=E[:], in1=Minc, op=ALU.mult), [E, cst], [DTi])
                        op("vector", lambda e: e.tensor_tensor(out=BBs[:], in0=psg[:, 128:256], in1=Mstr, op=ALU.mult), [psg, cst], [BBs])
                        op("gpsimd", lambda e: e.tensor_tensor(out=BBs[:], in0=BBs[:], in1=E[:], op=ALU.mult), [BBs, E], [BBs])
                        op("scalar", lambda e: e.activation(out=EB[:], in_=psg[:, 0:128], func=AF.Exp), [psg], [EB])
                        qg = m16.get()
                        op("gpsimd", lambda e: e.tensor_tensor(out=qg[:], in0=qT[:, js], in1=EB[:], op=ALU.mult), [qT, EB], [qg])
                        pk = self.psum()
                        op("tensor", lambda e: e.matmul(pk[:, 0:128], lhsT=kT[:, js], rhs=kT[:, js], start=True, stop=True), [kT], [pk])
                        op("tensor", lambda e: e.matmul(pk[:, 128:256], lhsT=kT[:, js], rhs=qT[:, js], start=True, stop=True), [kT, qT], [pk])
                        N = m32.get()
                        attT = m16.get()
                        op("vector", lambda e: e.tensor_tensor(out=N[:], in0=pk[:, 0:128], in1=BBs[:], op=ALU.mult), [pk, BBs], [N])
                        op("vector", lambda e: e.tensor_tensor(out=attT[:], in0=pk[:, 128:256], in1=DTi[:], op=ALU.mult), [pk, DTi], [attT])
                        pa = self.psum()
                        op("tensor", lambda e: e.matmul(pa[:, 0:128], lhsT=N[:], rhs=ident, start=True, stop=True), [N, cst], [pa])
                        A = m32.get()
                        op("scalar", lambda e: e.activation(out=A[:], in_=pa[:, 0:128], func=AF.Copy), [pa], [A])
                        TT = m32.get()
                        op("gpsimd", lambda e: e.tensor_tensor(out=TT[:], in0=ident, in1=N[:], op=ALU.subtract), [cst, N], [TT])
                        for k in range(1, 7):
                            p1 = self.psum()
                            op("tensor", lambda e: e.matmul(p1[:, 0:128], lhsT=N[:], rhs=A[:], start=True, stop=True), [N, A], [p1])
                            if k < 6:
                                op("tensor", lambda e: e.matmul(p1[:, 128:256], lhsT=A[:], rhs=N[:], start=True, stop=True), [N, A], [p1])
                            A2 = m32.get()
                            op("scalar", lambda e: e.activation(out=A2[:], in_=p1[:, 0:128], func=AF.Copy), [p1], [A2])
                            if k < 6:
                                N2 = m32.get()
                                op("vector", lambda e: e.tensor_copy(out=N2[:], in_=p1[:, 128:256]), [p1], [N2])
                                N = N2
                            A = A2
                            p2 = self.psum()
                            op("tensor", lambda e: e.matmul(p2[:, 0:128], lhsT=A[:], rhs=TT[:], start=True, stop=True), [A, TT], [p2])
                            TT2 = m32.get() if k < 6 else m16.get()
                            op("vector", lambda e: e.tensor_tensor(out=TT2[:], in0=p2[:, 0:128], in1=TT[:], op=ALU.add), [p2, TT], [TT2])
                            TT = TT2
                        vb, kbg, kdec = m16.get(), m16.get(), m16.get()
                        op("gpsimd", lambda e: e.tensor_scalar(out=vb[:], in0=v_tm[:, j, :], scalar1=beta[:, j, d * GH + h:d * GH + h + 1], scalar2=None, op0=ALU.mult), [v_tm, beta], [vb])
                        op("gpsimd", lambda e: e.tensor_scalar(out=kbg[:], in0=k_tm[:, j, :], scalar1=col(bk[d]), scalar2=None, op0=ALU.mult), [k_tm, bk[d]], [kbg])
                        op("gpsimd", lambda e: e.tensor_scalar(out=kdec[:], in0=k_tm[:, j, :], scalar1=col(kds[d]), scalar2=None, op0=ALU.mult), [k_tm, kds[d]], [kdec])
                        pu = self.psum()
                        op("tensor", lambda e: e.matmul(pu[:, 0:128], lhsT=TT[:], rhs=vb[:], start=True, stop=True), [TT, vb], [pu])
                        op("tensor", lambda e: e.matmul(pu[:, 128:256], lhsT=kbg[:], rhs=TT[:], start=True, stop=True), [TT, kbg], [pu])
                        u, wT = m32.get(), m16.get()
                        op("scalar", lambda e: e.activation(out=u[:], in_=pu[:, 0:128], func=AF.Copy), [pu], [u])
                        op("scalar", lambda e: e.activation(out=wT[:], in_=pu[:, 128:256], func=AF.Copy), [pu], [wT])
                        pv = self.psum()
                        op("tensor", lambda e: e.matmul(pv[:, 0:128], lhsT=wT[:], rhs=Sb[:], start=True, stop=True), [wT, Sb], [pv])
                        vn = m16.get()
                        op("vector", lambda e: e.tensor_tensor(out=vn[:], in0=u[:], in1=pv[:, 0:128], op=ALU.subtract), [u, pv], [vn])
                        po = self.psum()
                        op("tensor", lambda e: e.matmul(po[:, 0:128], lhsT=Sb[:], rhs=qg[:], start=True, stop=False), [Sb, qg], [po])
                        op("tensor", lambda e: e.matmul(po[:, 0:128], lhsT=vn[:], rhs=attT[:], start=False, stop=True), [vn, attT], [po])
                        op("tensor", lambda e: e.matmul(po[:, 128:256], lhsT=kdec[:], rhs=vn[:], start=True, stop=True), [kdec, vn], [po])
                        if d == 0:
                            op("scalar", lambda e: e.activation(out=Oacc[:, js], in_=po[:, 0:128], func=AF.Copy), [po], [Oacc])
                        else:
                            op("vector", lambda e: e.tensor_tensor(out=Oacc[:, js], in0=po[:, 0:128], in1=Oacc[:, js], op=ALU.add), [po, Oacc], [Oacc])
                        S2, Sb2 = Sp.get(), Sbp.get()
                        op("vector", lambda e: e.scalar_tensor_tensor(out=S2[:], in0=S32[:], scalar=col(egl[d]), in1=po[:, 128:256], op0=ALU.mult, op1=ALU.add), [S32, egl[d], po], [S2])
                        op("scalar", lambda e: e.activation(out=Sb2[:], in_=S2[:], func=AF.Copy), [S2], [Sb2])
                        S32, Sb = S2, Sb2
                for (t0, ts, isc) in cfg.blocks:
                    self.head_norm(Oacc, ts, nw, ggT, h * 128, cfg.DRET + h * 128, t0, src=Oacc[:, t0:t0 + ts])
            em.barrier()


def make_consts():
    p = np.arange(128, dtype=np.float32)[:, None]
    c = np.arange(128, dtype=np.float32)[None, :]
    C = np.zeros((128, 12, 128), np.float32)
    C[:, 0] = 1.0
    C[:, 1] = c - p
    C[:, 2] = (c >= p)
    C[:, 3] = p - c
    C[:, 4] = (p >= c)
    C[:, 5] = c + 1 + 0 * p
    C[:, 6] = 128 - c + 0 * p
    C[:, 7, 0] = 127 - p[:, 0]
    C[:, 7, 1] = p[:, 0]
    C[:, 7, 2] = 128.0
    C[:, 8] = (c == p)
    C[:, 9] = (c > p)
    C[:, 10] = (p > c)
    return C


def rope_tables(cfg, hd):
    L, T, CTX = cfg.SEQ, cfg.T, cfg.CTX
    m = hd // 2
    nf = m // 2
    t = np.arange(L)
    row = (t // 64).astype(np.float32)
    col = (t % 64).astype(np.float32)
    inv = (np.float32(10000.0) ** (-np.arange(nf, dtype=np.float32) / np.float32(nf))).astype(np.float32)
    ar = (row[:, None] * inv[None, :]).astype(np.float32)
    ac = (col[:, None] * inv[None, :]).astype(np.float32)
    ang = np.concatenate([ar, ar, ac, ac], axis=1)
    out = np.zeros((2, hd, T), np.float32)
    out[0, :, :CTX] = 1.0
    out[0, :, CTX:] = np.cos(ang).T
    out[1, :, CTX:] = np.sin(ang).T
    return out


def pchunks(v, reps=None):
    v = np.asarray(v, np.float32)
    k = v.shape[-1] // 128
    o = np.swapaxes(v.reshape(v.shape[:-1] + (k, 128)), -1, -2)
    if reps:
        o = np.repeat(o[..., None], reps, axis=-1)
    return np.ascontiguousarray(o)


def prep_shared(cfg, inp):
    L = cfg.DEPTH
    f = lambda a: np.ascontiguousarray(np.asarray(a, np.float32))
    sh = {}
    sh["ada_w"] = f(inp["ada_w"]); sh["w_in"] = f(inp["w_in"]); sh["w_qb"] = f(inp["mla_w_qb"])
    sh["w_kvb"] = f(inp["mla_w_kvb"]); sh["w_out"] = f(inp["w_out"])
    sh["ada_b"] = pchunks(inp["ada_b"], 2)
    sh["norm_w"] = pchunks(inp["norm_w"], 2)
    sh["fin_w"] = pchunks(inp["final_norm_w"])
    sh["qn_w"] = pchunks(inp["mla_q_norm_w"]); sh["kvn_w"] = pchunks(inp["mla_kv_norm_w"])
    sh["ret_nw"] = pchunks(inp["ret_norm_w"]); sh["gdn_nw"] = pchunks(inp["gdn_norm_w"])
    bc = lambda a: np.ascontiguousarray(np.broadcast_to(np.asarray(a, np.float32).reshape(L, 1, -1), (L, 128, np.asarray(a).shape[1] * np.asarray(a).shape[2])))
    sh["ret_ld"] = bc(inp["ret_log_decay"]); sh["gdn_al"] = bc(inp["gdn_a_log"]); sh["gdn_dtb"] = bc(inp["gdn_dt_bias"])
    cw = np.asarray(inp["gdn_conv_w"], np.float32)
    sh["conv_w"] = np.ascontiguousarray(np.transpose(cw.reshape(L, 5, 3 * cfg.GH, 128), (0, 3, 2, 1)))
    sh["rope_ret"] = rope_tables(cfg, 128); sh["rope_mla"] = rope_tables(cfg, 64)
    sh["consts"] = make_consts()
    return sh


def prep_core(cfg, inp, sh, b):
    m = dict(sh)
    x = np.asarray(inp["x"][b], np.float32); ctx = np.asarray(inp["ctx"][b], np.float32)
    m["xT"] = np.ascontiguousarray(np.concatenate([ctx, x], axis=0).T)
    cc = np.stack([np.asarray(inp["c"][b], np.float32), np.asarray(inp["c_ctx"], np.float32)], axis=-1)
    m["cc"] = np.ascontiguousarray(np.transpose(cc.reshape(cfg.KD, 128, 2), (1, 0, 2)))
    return m


_CACHE = {}


def kernel(**inputs):
    cfg = Cfg()
    if "nc" not in _CACHE:
        _CACHE["nc"] = Prog(cfg).build()
    nc = _CACHE["nc"]
    sh = prep_shared(cfg, inputs)
    B = np.asarray(inputs["x"]).shape[0]
    in_maps = [prep_core(cfg, inputs, sh, b) for b in range(B)]
    res = run_bass_kernel_spmd(nc, in_maps, core_ids=list(range(B)))
    out = np.stack([np.asarray(r["yT"]).T for r in res.results], axis=0)
    return np.ascontiguousarray(out.astype(np.float32))
```
